# Optimizing a Trainium2 kernel written in Bass

```python
import math
import jax, jax.numpy as jnp
from jax import lax
import numpy as np

D_MODEL = 1024
BATCH = 8
SEQ = 4096
DEPTH = 1

CHUNK = 64
PLE_DIM = 256
D_MIX = D_MODEL
CONV_WIDTH = D_MIX // 2
CONV_GROUPS = 8
CONV_K = 3
ATTN_WIDTH = D_MIX - CONV_WIDTH
N_HEADS = 4
HEAD_DIM = ATTN_WIDTH // (2 * N_HEADS)
V_DIM = 2 * HEAD_DIM
ROT_DIM = HEAD_DIM // 4
ROPE_THETA = 500000.0
Q_BLOCK = 128
EPS = 1e-6
SUBLN_EPS = 1e-5
SPLIT_SIZES = [CONV_WIDTH] * 4 + [ATTN_WIDTH] * 4
IN_COLS = sum(SPLIT_SIZES)

kernel_name = "hymba_conv_diffattn_ple_block"


def rmsnorm(x, g, eps=EPS):
    xf = x.astype(jnp.float32)
    y = xf * lax.rsqrt(jnp.mean(xf * xf, axis=-1, keepdims=True) + eps) * g.astype(jnp.float32)
    return y.astype(x.dtype)


def causal_depthwise_conv(u, w, b):
    c = u.shape[-1]
    y = lax.conv_general_dilated(
        u, w.astype(u.dtype)[:, None, :], window_strides=(1,),
        padding=[(CONV_K - 1, 0)], dimension_numbers=('NWC', 'WIO', 'NWC'),
        feature_group_count=c)
    return y + b.astype(u.dtype)


def partial_rope(x, positions):
    half = ROT_DIM // 2
    inv_freq = ROPE_THETA ** (-jnp.arange(half, dtype=jnp.float32) / half)
    ang = positions.astype(jnp.float32)[:, :, None] * inv_freq
    cos = jnp.cos(ang)[:, :, None, None, :]
    sin = jnp.sin(ang)[:, :, None, None, :]
    x1 = x[..., :half]
    x2 = x[..., half:ROT_DIM]
    rest = x[..., ROT_DIM:]
    return jnp.concatenate([x1 * cos - x2 * sin, x2 * cos + x1 * sin, rest], axis=-1)


def diff_attention(q, k, v, lam, positions):
    b, s = q.shape[0], q.shape[1]
    nb = s // Q_BLOCK
    qf = partial_rope(q.astype(jnp.float32), positions) * (HEAD_DIM ** -0.5)
    kf = partial_rope(k.astype(jnp.float32), positions)
    vf = v.astype(jnp.float32)
    q_blocks = qf.reshape(b, nb, Q_BLOCK, N_HEADS, 2, HEAD_DIM).swapaxes(0, 1)
    q_chunk = (jnp.arange(s) // CHUNK).reshape(nb, Q_BLOCK)
    k_chunk = jnp.arange(s) // CHUNK

    def one_block(args):
        qb, qc = args
        scores = jnp.einsum('bqhmd,bkhmd->bhmqk', qb, kf)
        mask = k_chunk[None, :] <= qc[:, None]
        scores = jnp.where(mask, scores, -jnp.inf)
        probs = jax.nn.softmax(scores, axis=-1)
        w = probs[:, :, 0] - lam * probs[:, :, 1]
        return jnp.einsum('bhqk,bkhd->bqhd', w, vf)

    out = lax.map(one_block, (q_blocks, q_chunk))
    return out.swapaxes(0, 1).reshape(b, s, N_HEADS, V_DIM)


def setup_inputs(seed: int = 0) -> dict:
    key = jax.random.key(seed)
    ks = jax.random.split(key, 20)
    f32 = jnp.float32
    x = jax.random.normal(ks[0], (BATCH, SEQ, D_MODEL), f32)
    p = jax.random.normal(ks[1], (DEPTH, BATCH, SEQ, PLE_DIM), f32)
    offsets = jax.random.randint(ks[2], (BATCH, 1), 0, 64, dtype=jnp.int32) * CHUNK
    positions = (offsets + jnp.arange(SEQ, dtype=jnp.int32)[None, :]).astype(jnp.int32)
    norm_mix = 1.0 + 0.02 * jax.random.normal(ks[3], (DEPTH, D_MODEL), f32)
    w_in = jax.random.normal(ks[4], (DEPTH, D_MODEL, IN_COLS), f32) * D_MODEL ** -0.5
    conv_w = jax.random.normal(ks[5], (DEPTH, CONV_K, CONV_WIDTH), f32) * CONV_K ** -0.5
    conv_b = 0.01 * jax.random.normal(ks[6], (DEPTH, CONV_WIDTH), f32)
    lambda_q1 = 0.1 * jax.random.normal(ks[7], (DEPTH, HEAD_DIM), f32)
    lambda_k1 = 0.1 * jax.random.normal(ks[8], (DEPTH, HEAD_DIM), f32)
    lambda_q2 = 0.1 * jax.random.normal(ks[9], (DEPTH, HEAD_DIM), f32)
    lambda_k2 = 0.1 * jax.random.normal(ks[10], (DEPTH, HEAD_DIM), f32)
    subln_g = 1.0 + 0.02 * jax.random.normal(ks[11], (DEPTH, V_DIM), f32)
    w_out = jax.random.normal(ks[12], (DEPTH, D_MIX, D_MODEL), f32) * D_MIX ** -0.5
    norm_ple = 1.0 + 0.02 * jax.random.normal(ks[13], (DEPTH, D_MODEL), f32)
    w_ple_gate = jax.random.normal(ks[14], (DEPTH, D_MODEL, D_MODEL), f32) * D_MODEL ** -0.5
    w_ple_proj = jax.random.normal(ks[15], (DEPTH, PLE_DIM, D_MODEL), f32) * PLE_DIM ** -0.5
    final_norm = 1.0 + 0.02 * jax.random.normal(ks[16], (D_MODEL,), f32)
    return {"x": x, "p": p, "positions": positions, "norm_mix": norm_mix, "w_in": w_in,
            "conv_w": conv_w, "conv_b": conv_b, "lambda_q1": lambda_q1, "lambda_k1": lambda_k1,
            "lambda_q2": lambda_q2, "lambda_k2": lambda_k2, "subln_g": subln_g, "w_out": w_out,
            "norm_ple": norm_ple, "w_ple_gate": w_ple_gate, "w_ple_proj": w_ple_proj,
            "final_norm": final_norm}


def reference(x, p, positions, norm_mix, w_in, conv_w, conv_b, lambda_q1, lambda_k1,
              lambda_q2, lambda_k2, subln_g, w_out, norm_ple, w_ple_gate, w_ple_proj,
              final_norm):
    b, s, _ = x.shape
    split_idx = np.cumsum(SPLIT_SIZES)[:-1].tolist()
    h = x
    for i in range(DEPTH):
        lam_init = 0.8 - 0.6 * math.exp(-0.3 * i)
        u = rmsnorm(h, norm_mix[i])
        proj = u @ w_in[i]
        cx, cb, cc, cz, q, k, v, az = jnp.split(proj, split_idx, axis=-1)

        y_conv = cb * causal_depthwise_conv(cc * cx, conv_w[i], conv_b[i]) * jax.nn.silu(cz)

        lam = (jnp.exp(jnp.sum(lambda_q1[i].astype(jnp.float32) * lambda_k1[i].astype(jnp.float32)))
               - jnp.exp(jnp.sum(lambda_q2[i].astype(jnp.float32) * lambda_k2[i].astype(jnp.float32)))
               + lam_init)
        q = q.reshape(b, s, N_HEADS, 2, HEAD_DIM)
        k = k.reshape(b, s, N_HEADS, 2, HEAD_DIM)
        v = v.reshape(b, s, N_HEADS, V_DIM)
        o = diff_attention(q, k, v, lam, positions)
        o = rmsnorm(o, subln_g[i], SUBLN_EPS) * (1.0 - lam_init)
        y_attn = o.reshape(b, s, ATTN_WIDTH).astype(h.dtype) * jax.nn.silu(az)

        h = h + jnp.concatenate([y_conv, y_attn], axis=-1) @ w_out[i]

        gate = jax.nn.sigmoid(rmsnorm(h, norm_ple[i]) @ w_ple_gate[i])
        h = h + gate * (p[i].astype(h.dtype) @ w_ple_proj[i])
    return rmsnorm(h, final_norm)
```

```python
import math
import os
DBG = int(os.environ.get('KDBG', '99'))
SKIP = set(os.environ.get('KSKIP', '').split(','))
from contextlib import ExitStack

import numpy as np
import concourse.bass as bass
import concourse.mybir as mybir
from concourse.bass_utils import run_bass_kernel_spmd

F32 = mybir.dt.float32
BF16 = mybir.dt.bfloat16
I32 = mybir.dt.int32
AF = mybir.ActivationFunctionType
ALU = mybir.AluOpType

D = 1024
KD = D // 128
PLE = 256
CW = 512
NH = 4
VD = 128
TB = 256
TBT = TB // 128
EPS = 1e-6
SUBLN_EPS = 1e-5
LAM_INIT = 0.8 - 0.6 * math.exp(-0.3 * 0)
TWO_PI = 2.0 * math.pi


class Tracker:
    def __init__(self, nc, es):
        self.nc = nc
        self.es = es
        self.eng = {}
        for name, h in (("pe", nc.tensor), ("act", nc.scalar), ("dve", nc.vector),
                        ("pool", nc.gpsimd), ("sp", nc.sync)):
            sem = es.enter_context(nc.semaphore("sem_" + name))
            self.eng[name] = dict(h=h, sem=sem, key="E" + name, count=0, seen={}, pend=[])
        self.sems = {e["key"]: e["sem"] for e in self.eng.values()}
        self.res = {}
        self.dma_sems = {}

    def dma_sem(self, name):
        if name not in self.dma_sems:
            sem = self.es.enter_context(self.nc.semaphore("dsem_" + name))
            key = "D" + name
            self.sems[key] = sem
            self.dma_sems[name] = dict(sem=sem, key=key, count=0)
        return self.dma_sems[name]

    def _deps(self, e, reads, writes):
        deps = {}

        def add(tok):
            if tok is None:
                return
            k, v = tok
            if deps.get(k, 0) < v:
                deps[k] = v
        for r in reads:
            st = self.res.get(r)
            if st:
                add(st["w"])
        for w in writes:
            st = self.res.get(w)
            if st:
                add(st["w"])
                for k, v in st["r"].items():
                    add((k, v))
        for k, v in sorted(deps.items()):
            if e["key"] == "Epe" and k == "Epe":
                continue
            if e["seen"].get(k, 0) < v:
                e["h"].wait_ge(self.sems[k], v)
                e["seen"][k] = v

    def _commit(self, tok, reads, writes):
        k, v = tok
        for r in reads:
            st = self.res.setdefault(r, dict(w=None, r={}))
            if st["r"].get(k, 0) < v:
                st["r"][k] = v
        for w in writes:
            self.res[w] = dict(w=tok, r={})

    section = ""

    def op(self, engname, fn, reads=(), writes=(), inc=True, tag=""):
        if (self.section + ":" + engname) in SKIP or (self.section + ":" + tag) in SKIP:
            return None
        e = self.eng[engname]
        self._deps(e, reads, writes)
        ins = fn(e["h"])
        if inc:
            e["count"] += 1
            ins.then_inc(e["sem"], 1)
            tok = (e["key"], e["count"])
            self._commit(tok, reads, writes)
            for (r, w) in e["pend"]:
                self._commit(tok, r, w)
            e["pend"] = []
        else:
            e["pend"].append((tuple(reads), tuple(writes)))
        return ins

    def dma(self, queue, semname, pairs, reads=(), writes=(), **kw):
        e = self.eng[queue]
        ds = self.dma_sem(semname)
        self._deps(e, reads, writes)
        for (o, i) in pairs:
            e["h"].dma_start(out=o, in_=i, **kw).then_inc(ds["sem"], 16)
            ds["count"] += 16
        tok = (ds["key"], ds["count"])
        self._commit(tok, reads, writes)

    def wait_all(self, engname):
        e = self.eng[engname]
        for o in self.eng.values():
            if o["count"] > 0 and e["seen"].get(o["key"], 0) < o["count"]:
                e["h"].wait_ge(o["sem"], o["count"])
                e["seen"][o["key"]] = o["count"]
        for ds in self.dma_sems.values():
            if ds["count"] > 0 and e["seen"].get(ds["key"], 0) < ds["count"]:
                e["h"].wait_ge(ds["sem"], ds["count"])
                e["seen"][ds["key"]] = ds["count"]


def build_nc(S):
    NT = S // 128
    NB = S // TB
    nc = bass.Bass("TRN2", target_bir_lowering=False)

    def din(name, shape, dt=F32):
        return nc.dram_tensor(name, shape, dt, kind="ExternalInput").ap()

    x_d = din("x", [S, D])
    p_d = din("p", [S, PLE])
    pos_d = din("pos_t", [128, NT], I32)
    gmix_d = din("gmix", [128, KD])
    gple_d = din("gple", [128, KD])
    win_d = din("w_in", [D, 4096])
    convw_d = din("convw", [128, 4, 3])
    convb_d = din("convb", [128, 4])
    lam_d = din("lam4", [4, 64])
    subg_d = din("subg", [VD])
    wout_d = din("w_out", [D, D])
    wgate_d = din("w_gate", [D, D])
    wple_d = din("w_ple", [PLE, D])
    fnorm_d = din("fnorm", [D])
    ident_d = din("ident", [128, 128])
    invf_d = din("invf", [8])
    out_d = nc.dram_tensor("out", [S, D], F32, kind="ExternalOutput").ap()

    es = ExitStack()
    with es:
        T = Tracker(nc, es)

        def sb(name, shape, dt):
            return es.enter_context(nc.sbuf_tensor(name, shape, dt))

        KT = sb("KT", [128, NH, S], BF16)
        VA = sb("VA", [128, NT, NH, 130], BF16)
        W2 = sb("W2", [128, KD, 3072], BF16)
        WOUT = sb("WOUT", [128, KD, D], BF16)
        WGATE = sb("WGATE", [128, KD, D], BF16)
        WPLE = sb("WPLE", [128, 2, D], BF16)
        fnb = sb("fnb", [128, D], F32)
        identb = sb("identb", [128, 128], BF16)
        cosT = sb("cosT", [128, NT, 8], F32)
        sinT = sb("sinT", [128, NT, 8], F32)
        convw = sb("convw_s", [128, 4, 3], F32)
        convb = sb("convb_s", [128, 4], F32)
        gsb = sb("gsb", [128, VD], F32)
        neglam = sb("neglam", [128, 1], F32)
        gmix = sb("gmix_s", [128, KD], F32)
        gple = sb("gple_s", [128, KD], F32)
        ss = sb("ss", [128, 1], F32)
        lnv = sb("lnv", [128, 1], F32)
        rstd = sb("rstd", [128, 1], F32)
        Fb = [sb("F0", [128, D], F32), sb("F1", [128, D], F32)]
        B1 = sb("B1", [128, D], BF16)
        uT = sb("uT", [128, KD, 128], BF16)
        rtmp = sb("rtmp", [128, 4, 8, 8], F32)
        qk_tm = sb("qk_tm", [128, 512], BF16)

        PS = es.enter_context(nc.psum_tensor("PS", [128, 8, 512], F32))

        def bank_bf(b):
            return PS[:, b, :].bitcast(BF16)

        def bkeys(*bs):
            return tuple("ps%d" % b for b in bs)

        cstate = dict(f=0)

        with ExitStack() as es0:
            def sb0(name, shape, dt):
                return es0.enter_context(nc.sbuf_tensor(name, shape, dt))
            WKV = sb0("WKV", [128, KD, 1024], BF16)
            stg = [sb0("stg0", [128, 1024], F32), sb0("stg1", [128, 1024], F32)]
            identf = sb0("identf", [128, 128], F32)
            posi = sb0("posi", [128, NT], I32)
            posf = sb0("posf", [128, NT], F32)
            invf = sb0("invf_s", [128, 8], F32)
            ang = sb0("ang", [128, NT, 8], F32)
            tq = sb0("tq", [128, NT, 8], F32)
            ki = sb0("ki", [128, NT, 8], I32)
            kf = sb0("kf", [128, NT, 8], F32)
            lamb = sb0("lamb", [128, 2, 2, 64], F32)
            lprod = sb0("lprod", [128, 2, 64], F32)
            lsum = sb0("lsum", [128, 2], F32)
            subg_s = sb0("subg_s", [128, VD], F32)

            small = "c_small"
            T.dma("sp", small, [(identf[:], ident_d[:, :])], writes=["identf"])
            T.dma("sp", small, [(posi[:], pos_d[:, :])], writes=["posi"])
            T.dma("sp", small, [(invf[:], invf_d.partition_broadcast(128))], writes=["invf"])
            T.dma("sp", small, [(gmix[:], gmix_d[:, :])], writes=["gmix"])
            T.dma("sp", small, [(gple[:], gple_d[:, :])], writes=["gple"])
            T.dma("sp", small, [(convw[:], convw_d[:, :, :])], writes=["convw"])
            T.dma("sp", small, [(convb[:], convb_d[:, :])], writes=["convb"])
            T.dma("sp", small, [(lamb[:].rearrange("p a b d -> p (a b d)"),
                                 lam_d.rearrange("a b -> (a b)").partition_broadcast(128))], writes=["lamb"])
            T.dma("sp", small, [(subg_s[:], subg_d.partition_broadcast(128))], writes=["subg_s"])
            T.dma("sp", small, [(fnb[:], fnorm_d.partition_broadcast(128))], writes=["fnb"])
            allsmall = ["identf", "posi", "invf", "gmix", "gple", "convw", "convb", "lamb", "subg_s", "fnb"]
            tok = T.res["fnb"]["w"]
            for r in allsmall:
                T.res[r]["w"] = tok

            T.op("dve", lambda v: v.tensor_copy(out=identb[:], in_=identf[:]), reads=["identf"], writes=["identb"])
            T.op("dve", lambda v: v.memset(VA[:, :, :, 128:129], 1.0), writes=["VAones"])
            T.op("dve", lambda v: v.tensor_scalar(out=gsb[:], in0=subg_s[:], scalar1=float(1.0 - LAM_INIT),
                                                  scalar2=None, op0=ALU.mult), reads=["subg_s"], writes=["gsb"])
            T.op("dve", lambda v: v.tensor_tensor(out=lprod[:], in0=lamb[:, :, 0, :], in1=lamb[:, :, 1, :], op=ALU.mult),
                 reads=["lamb"], writes=["lprod"])
            T.op("dve", lambda v: v.reduce_sum(out=lsum[:], in_=lprod[:], axis=mybir.AxisListType.X),
                 reads=["lprod"], writes=["lsum"])
            T.op("act", lambda a: a.activation(out=lsum[:], in_=lsum[:], func=AF.Exp), reads=["lsum"], writes=["lsum"])
            T.op("dve", lambda v: v.scalar_tensor_tensor(out=neglam[:], in0=lsum[:, 1:2], scalar=float(-LAM_INIT),
                                                         in1=lsum[:, 0:1], op0=ALU.add, op1=ALU.subtract),
                 reads=["lsum"], writes=["neglam"])

            T.op("dve", lambda v: v.tensor_copy(out=posf[:], in_=posi[:]), reads=["posi"], writes=["posf"])
            T.op("dve", lambda v: v.tensor_tensor(out=ang[:], in0=posf[:].unsqueeze(2).broadcast_to([128, NT, 8]),
                                                  in1=invf[:].unsqueeze(1).broadcast_to([128, NT, 8]), op=ALU.mult),
                 reads=["posf", "invf"], writes=["ang"])
            for (dst, shift, nm) in ((sinT, 0.0, "sinT"), (cosT, 0.25, "cosT")):
                T.op("dve", lambda v: v.tensor_scalar(out=tq[:], in0=ang[:], scalar1=float(1.0 / TWO_PI), scalar2=float(shift),
                                                      op0=ALU.mult, op1=ALU.add), reads=["ang"], writes=["tq"])
                T.op("dve", lambda v: v.tensor_copy(out=ki[:], in_=tq[:]), reads=["tq"], writes=["ki"])
                T.op("dve", lambda v: v.tensor_copy(out=kf[:], in_=ki[:]), reads=["ki"], writes=["kf"])
                T.op("dve", lambda v: v.tensor_tensor(out=tq[:], in0=tq[:], in1=kf[:], op=ALU.subtract),
                     reads=["tq", "kf"], writes=["tq"])
                T.op("dve", lambda v: v.tensor_single_scalar(out=kf[:], in_=tq[:], scalar=0.5, op=ALU.is_gt),
                     reads=["tq"], writes=["kf"])
                T.op("dve", lambda v: v.tensor_tensor(out=tq[:], in0=tq[:], in1=kf[:], op=ALU.subtract),
                     reads=["tq", "kf"], writes=["tq"])
                T.op("dve", lambda v: v.tensor_single_scalar(out=kf[:], in_=tq[:], scalar=-0.5, op=ALU.is_lt),
                     reads=["tq"], writes=["kf"])
                T.op("dve", lambda v: v.tensor_tensor(out=tq[:], in0=tq[:], in1=kf[:], op=ALU.add),
                     reads=["tq", "kf"], writes=["tq"])
                T.op("act", lambda a, dst=dst: a.activation(out=dst[:], in_=tq[:], func=AF.Sin, scale=6.283185),
                     reads=["tq"], writes=[nm])

            cvt_engs = ["dve", "pool", "act"]
            wstate = dict(i=0, chunk=0)

            def cvt(dst, src, scale_ap, reads, writes):
                en = cvt_engs[wstate["i"] % 3]
                wstate["i"] += 1
                if scale_ap is None:
                    if en == "act":
                        T.op("act", lambda a: a.copy(out=dst, in_=src), reads=reads, writes=writes)
                    else:
                        T.op(en, lambda v: v.tensor_copy(out=dst, in_=src), reads=reads, writes=writes)
                else:
                    if en == "act":
                        T.op("act", lambda a: a.activation(out=dst, in_=src, func=AF.Copy, scale=scale_ap),
                             reads=reads + ["gmix", "gple"], writes=writes)
                    else:
                        T.op(en, lambda v: v.tensor_scalar(out=dst, in0=src, scalar1=scale_ap, scalar2=None, op0=ALU.mult),
                             reads=reads + ["gmix", "gple"], writes=writes)

            def load_chunk(pairs_fn):
                c = wstate["chunk"]
                wstate["chunk"] += 1
                s = c % 2
                return s, "stg%d" % s

            for kk in range(KD):
                rows = slice(kk * 128, (kk + 1) * 128)
                for c4 in range(4):
                    s, key = load_chunk(None)
                    T.dma("sp", key, [(stg[s][:], win_d[rows, c4 * 1024:(c4 + 1) * 1024])], writes=[key])
                    for hh in range(2):
                        col = c4 * 2 + hh
                        src = stg[s][:, hh * 512:(hh + 1) * 512]
                        if col < 4:
                            cvt(W2[:, kk, 1024 + col * 512:1024 + (col + 1) * 512], src, gmix[:, kk:kk + 1], [key], ["W2_%d_%d" % (kk, 2 + col)])
                        elif col == 4:
                            cvt(W2[:, kk, 0:512], src, gmix[:, kk:kk + 1], [key], ["W2_%d_0" % kk])
                        elif col == 5:
                            cvt(WKV[:, kk, 0:512], src, gmix[:, kk:kk + 1], [key], ["WKV_%d_0" % kk])
                        elif col == 6:
                            cvt(WKV[:, kk, 512:1024], src, gmix[:, kk:kk + 1], [key], ["WKV_%d_1" % kk])
                        else:
                            cvt(W2[:, kk, 512:1024], src, gmix[:, kk:kk + 1], [key], ["W2_%d_1" % kk])
            for (Wd, Wsb, gcol, nm, nk) in ((wout_d, WOUT, None, "WOUT", KD), (wgate_d, WGATE, gple, "WGATE", KD), (wple_d, WPLE, None, "WPLE", 2)):
                for kk in range(nk):
                    s, key = load_chunk(None)
                    T.dma("sp", key, [(stg[s][:], Wd[kk * 128:(kk + 1) * 128, :])], writes=[key])
                    for hh in range(2):
                        cvt(Wsb[:, kk, hh * 512:(hh + 1) * 512], stg[s][:, hh * 512:(hh + 1) * 512],
                            None if gcol is None else gcol[:, kk:kk + 1], [key], ["%s_%d_%d" % (nm, kk, hh)])

            def load_x(t):
                s = cstate["f"] % 2
                cstate["f"] += 1
                T.dma("sp", "F%d" % s, [(Fb[s][:], x_d[t * 128:(t + 1) * 128, :])], writes=["F%d" % s])
                return s

            def rms_rstd(src_ap, src_keys, n, eps):
                T.op("act", lambda a: a.activation(out=B1[:, 0:n], in_=src_ap, func=AF.Square, accum_out=ss[:]),
                     reads=src_keys, writes=["B1", "ss"])
                T.op("act", lambda a: a.activation(out=lnv[:], in_=ss[:], func=AF.Ln, scale=1.0 / n, bias=float(eps)),
                     reads=["ss"], writes=["lnv"])
                T.op("act", lambda a: a.activation(out=rstd[:], in_=lnv[:], func=AF.Exp, scale=-0.5),
                     reads=["lnv"], writes=["rstd"])

            def norm_transpose(s):
                fk = "F%d" % s
                rms_rstd(Fb[s][:], [fk], D, EPS)
                T.op("dve", lambda v: v.tensor_scalar(out=B1[:], in0=Fb[s][:], scalar1=rstd[:, 0:1], scalar2=None, op0=ALU.mult),
                     reads=[fk, "rstd"], writes=["B1"])
                pst = bank_bf(6).rearrange("p (k n) -> p k n", k=KD)
                for kk in range(KD):
                    T.op("pe", lambda pe, kk=kk: pe.transpose(out=pst[:, kk, :], in_=B1[:, kk * 128:(kk + 1) * 128], identity=identb[:]),
                         reads=["B1", "identb"], writes=bkeys(6), inc=(kk == KD - 1))
                T.op("dve", lambda v: v.tensor_copy(out=uT[:], in_=pst), reads=bkeys(6), writes=["uT"])

            def rope_to(psrc, t, dst_key):
                src3 = psrc.rearrange("p (g d) -> p g d", g=8)
                dst3 = qk_tm[:].rearrange("p (g d) -> p g d", g=8)
                cb_ = cosT[:, t, :].unsqueeze(1).broadcast_to([128, 8, 8])
                sb_ = sinT[:, t, :].unsqueeze(1).broadcast_to([128, 8, 8])
                x1 = src3[:, :, 0:8]
                x2 = src3[:, :, 8:16]
                T.op("dve", lambda v: v.tensor_copy(out=dst3[:, :, 16:64], in_=src3[:, :, 16:64]), reads=[dst_key], writes=["qk_rest"], tag="rope")
                T.op("dve", lambda v: v.tensor_tensor(out=rtmp[:, 0], in0=x1, in1=cb_, op=ALU.mult), reads=[dst_key, "cosT"], writes=["rt0"], tag="rope")
                T.op("dve", lambda v: v.tensor_tensor(out=rtmp[:, 1], in0=x2, in1=sb_, op=ALU.mult), reads=[dst_key, "sinT"], writes=["rt1"], tag="rope")
                T.op("dve", lambda v: v.tensor_tensor(out=rtmp[:, 2], in0=x2, in1=cb_, op=ALU.mult), reads=[dst_key, "cosT"], writes=["rt2"], tag="rope")
                T.op("dve", lambda v: v.tensor_tensor(out=rtmp[:, 3], in0=x1, in1=sb_, op=ALU.mult), reads=[dst_key, "sinT"], writes=["rt3"], tag="rope")
                T.op("dve", lambda v: v.tensor_tensor(out=dst3[:, :, 0:8], in0=rtmp[:, 0], in1=rtmp[:, 1], op=ALU.subtract),
                     reads=["rt0", "rt1"], writes=["qk_r1"], tag="rope")
                T.op("dve", lambda v: v.tensor_tensor(out=dst3[:, :, 8:16], in0=rtmp[:, 2], in1=rtmp[:, 3], op=ALU.add),
                     reads=["rt2", "rt3"], writes=["qk_r2"], tag="rope")

            QK_KEYS = ["qk_rest", "qk_r1", "qk_r2"]

            for t in range(NT if DBG >= 1 else 0):
                s = load_x(t)
                norm_transpose(s)
                for kk in range(KD):
                    for g in range(2):
                        T.op("pe", lambda pe, kk=kk, g=g: pe.matmul(PS[:, g, :], lhsT=uT[:, kk, :], rhs=WKV[:, kk, g * 512:(g + 1) * 512],
                                                                  start=(kk == 0), stop=(kk == KD - 1)),
                             reads=["uT", "WKV_%d_%d" % (kk, g)], writes=bkeys(g), inc=(kk == KD - 1))
                T.op("act", lambda a, t=t: a.copy(out=VA[:, t, :, 0:128], in_=PS[:, 1, :].rearrange("p (h d) -> p h d", h=NH)),
                     reads=bkeys(1), writes=["VA%d" % t])
                rope_to(PS[:, 0, :], t, "ps0")
                pst = bank_bf(7).rearrange("p (k n) -> p k n", k=KD)
                for h in range(NH):
                    T.op("pe", lambda pe, h=h: pe.transpose(out=pst[:, h, :], in_=qk_tm[:, h * 128:(h + 1) * 128], identity=identb[:]),
                         reads=QK_KEYS + ["identb"], writes=bkeys(7), inc=(h == NH - 1))
                T.op("dve", lambda v, t=t: v.tensor_copy(out=KT[:, :, t * 128:(t + 1) * 128], in_=pst[:, 0:NH, :]),
                     reads=bkeys(7), writes=["KT%d" % t])

            for en in ("pe", "act", "dve", "pool", "sp"):
                T.wait_all(en)

        saz = sb("saz", [128, TBT, 512], BF16)
        QTz = sb("QTz", [128, NH, 2, TB], BF16)
        mixT = sb("mixT", [128, KD, TB], BF16)
        PT = [sb("PT0", [128, 2, 2, TB], BF16), sb("PT1", [128, 2, 2, TB], BF16)]
        sz = sb("sz", [128, 512], BF16)
        p_tm = sb("p_tm", [128, 512], BF16)
        g_tm = sb("g_tm", [128, 512], BF16)
        Pbuf = sb("Pbuf", [128, 4, 130], F32)
        acc = sb("acc", [128, 4, 128], F32)
        ctmp = sb("ctmp", [128, 4, 128], F32)
        ytm = sb("ytm", [128, TBT, NH, VD], BF16)
        rl = sb("rl", [128, 2], F32)
        o_t = sb("o_t", [128, VD], F32)
        o_u = sb("o_u", [128, VD], F32)
        gate = sb("gate", [128, D], F32)
        sg = gate[:, 0:512]
        cxs = gate[:, 512:1024]
        pst_f = sb("pst_f", [128, PLE], F32)
        pst_b = sb("pst_b", [128, PLE], BF16)
        pT = sb("pT", [128, 2, 128], BF16)

        print("SBUF bytes remaining at pass-2 peak:", nc.sbuf_bytes_remaining)
        T.op("pool", lambda g: g.memset(QTz[:], 0.0), writes=["QTz_lo", "QTz_hi"])
        T.op("pool", lambda g: g.memset(Pbuf[:], 0.0), writes=["Pbuf_hist", "Pbuf_main"])

        def sigmoid_to(dst_ap, dst_keys, src_ap, src_keys, tag="sig"):
            T.op("act", lambda a: a.activation(out=dst_ap, in_=src_ap, func=AF.Exp, scale=-1.0), reads=src_keys, writes=dst_keys, tag=tag)
            T.op("act", lambda a: a.activation(out=dst_ap, in_=dst_ap, func=AF.Ln, bias=1.0), reads=dst_keys, writes=dst_keys, tag=tag)
            T.op("act", lambda a: a.activation(out=dst_ap, in_=dst_ap, func=AF.Exp, scale=-1.0), reads=dst_keys, writes=dst_keys, tag=tag)

        def w2keys(g):
            return ["W2_%d_%d" % (kk, g) for kk in range(KD)]

        for b in range(NB if DBG >= 2 else 0):
            for j in range(TBT):
                t = b * TBT + j
                s = load_x(t)
                norm_transpose(s)
                T.section = "p2"
                for kk in range(KD):
                    for g in range(6):
                        T.op("pe", lambda pe, kk=kk, g=g: pe.matmul(PS[:, g, :], lhsT=uT[:, kk, :], rhs=W2[:, kk, g * 512:(g + 1) * 512],
                                                                  start=(kk == 0), stop=(kk == KD - 1)),
                             reads=["uT", "W2_%d_%d" % (kk, g)], writes=bkeys(g), inc=(kk == KD - 1))
                rope_to(PS[:, 0, :], t, "ps0")
                pst = bank_bf(6).rearrange("p (k n) -> p k n", k=KD)
                for h in range(NH):
                    T.op("pe", lambda pe, h=h: pe.transpose(out=pst[:, h, :], in_=qk_tm[:, h * 128:(h + 1) * 128], identity=identb[:]),
                         reads=QK_KEYS + ["identb"], writes=bkeys(6), inc=(h == NH - 1), tag="qtr")
                T.op("dve", lambda v, j=j: v.tensor_copy(out=QTz[0:64, :, 0, j * 128:(j + 1) * 128], in_=pst[0:64, 0:NH, :]),
                     reads=bkeys(6), writes=["QTz_lo"], tag="qtz")
                T.op("act", lambda a, j=j: a.copy(out=QTz[64:128, :, 1, j * 128:(j + 1) * 128], in_=pst[64:128, 0:NH, :]),
                     reads=bkeys(6), writes=["QTz_hi"], tag="qtz")
                sigmoid_to(sg, ["gate"], PS[:, 1, :], bkeys(1))
                T.op("dve", lambda v, j=j: v.tensor_tensor(out=saz[:, j, :], in0=PS[:, 1, :], in1=sg, op=ALU.mult),
                     reads=bkeys(1) + ("gate",), writes=["saz%d" % j], tag="sazmul")
                sigmoid_to(sg, ["gate"], PS[:, 5, :], bkeys(5))
                T.op("dve", lambda v: v.tensor_tensor(out=sz[:], in0=PS[:, 5, :], in1=sg, op=ALU.mult),
                     reads=bkeys(5) + ("gate",), writes=["sz"], tag="szmul")
                T.op("act", lambda a: a.copy(out=cxs, in_=PS[:, 2, :]), reads=bkeys(2), writes=["cxs"], tag="cxs")
                T.op("dve", lambda v: v.tensor_tensor(out=p_tm[:], in0=PS[:, 4, :], in1=cxs, op=ALU.mult),
                     reads=bkeys(4) + ("cxs",), writes=["p_tm"], tag="ptm")
                T.op("dve", lambda v: v.tensor_tensor(out=g_tm[:], in0=PS[:, 3, :], in1=sz[:], op=ALU.mult),
                     reads=bkeys(3) + ("sz",), writes=["g_tm"], tag="gtm")
                pst7 = bank_bf(7).rearrange("p (k n) -> p k n", k=KD)
                for c in range(4):
                    T.op("pe", lambda pe, c=c: pe.transpose(out=pst7[:, c, :], in_=p_tm[:, c * 128:(c + 1) * 128], identity=identb[:]),
                         reads=["p_tm", "identb"], writes=bkeys(7), inc=False, tag="ctr")
                for c in range(4):
                    T.op("pe", lambda pe, c=c: pe.transpose(out=pst7[:, 4 + c, :], in_=g_tm[:, c * 128:(c + 1) * 128], identity=identb[:]),
                         reads=["g_tm", "identb"], writes=bkeys(7), inc=(c == 3), tag="ctr")
                T.op("act", lambda a: a.copy(out=Pbuf[:, :, 2:130], in_=pst7[:, 0:4, :]), reads=bkeys(7), writes=["Pbuf_main"], tag="pbuf")
                def wb(k):
                    return convw[:, :, k:k + 1].broadcast_to([128, 4, 128])
                T.op("pool", lambda g: g.tensor_tensor(out=acc[:], in0=Pbuf[:, :, 2:130], in1=wb(2), op=ALU.mult),
                     reads=["Pbuf_main", "convw"], writes=["acc"])
                T.op("pool", lambda g: g.tensor_tensor(out=ctmp[:], in0=Pbuf[:, :, 1:129], in1=wb(1), op=ALU.mult),
                     reads=["Pbuf_main", "Pbuf_hist", "convw"], writes=["ctmp"])
                T.op("pool", lambda g: g.tensor_tensor(out=acc[:], in0=acc[:], in1=ctmp[:], op=ALU.add),
                     reads=["acc", "ctmp"], writes=["acc"])
                T.op("pool", lambda g: g.tensor_tensor(out=ctmp[:], in0=Pbuf[:, :, 0:128], in1=wb(0), op=ALU.mult),
                     reads=["Pbuf_main", "Pbuf_hist", "convw"], writes=["ctmp"])
                T.op("pool", lambda g: g.tensor_tensor(out=acc[:], in0=acc[:], in1=ctmp[:], op=ALU.add),
                     reads=["acc", "ctmp"], writes=["acc"])
                T.op("pool", lambda g: g.tensor_tensor(out=acc[:], in0=acc[:], in1=convb[:].unsqueeze(2).broadcast_to([128, 4, 128]), op=ALU.add),
                     reads=["acc", "convb"], writes=["acc"])
                T.op("dve", lambda v, j=j: v.tensor_tensor(out=mixT[:, 0:4, j * 128:(j + 1) * 128], in0=acc[:], in1=pst7[:, 4:8, :], op=ALU.mult),
                     reads=["acc"] + list(bkeys(7)), writes=["mixTc%d" % j], tag="mixc")
                T.op("pool", lambda g: g.tensor_copy(out=Pbuf[:, :, 0:2], in_=Pbuf[:, :, 128:130]),
                     reads=["Pbuf_main"], writes=["Pbuf_hist"])

            T.section = ""
            npairs = b + 1
            slot = 0
            for h in range(NH if DBG >= 3 else 0):
                for pr in range(npairs):
                    diag = (pr == npairs - 1)
                    ssl = slot % 2
                    slot += 1
                    psS = PS[:, 2 * ssl:2 * ssl + 2, :].rearrange("p b (m q) -> p b m q", m=2)
                    sk = bkeys(2 * ssl, 2 * ssl + 1)
                    ptk = "PT%d" % ssl
                    n_mm = 0
                    for kt2 in range(2):
                        kt = 2 * pr + kt2
                        q0 = 128 if (diag and kt2 == 1) else 0
                        for m in range(2):
                            n_mm += 1
                            T.op("pe", lambda pe, kt=kt, kt2=kt2, m=m, q0=q0, h=h: pe.matmul(
                                psS[:, kt2, m, q0:TB], lhsT=KT[:, h, kt * 128:(kt + 1) * 128], rhs=QTz[:, h, m, q0:TB],
                                start=True, stop=True),
                                reads=["KT%d" % kt, "QTz_lo", "QTz_hi"], writes=sk, inc=(n_mm == 4))
                    pka, pkb = ptk + "a", ptk + "b"
                    if not diag:
                        T.op("act", lambda a, ssl=ssl, psS=psS: a.activation(out=PT[ssl][:], in_=psS, func=AF.Exp, scale=0.125),
                             reads=sk, writes=[pka, pkb])
                    else:
                        T.op("act", lambda a, ssl=ssl, psS=psS: a.activation(out=PT[ssl][:, 0], in_=psS[:, 0], func=AF.Exp, scale=0.125),
                             reads=sk, writes=[pka])
                        T.op("act", lambda a, ssl=ssl, psS=psS: a.activation(out=PT[ssl][:, 1, :, 128:TB], in_=psS[:, 1, :, 128:TB],
                                                                           func=AF.Exp, scale=0.125),
                             reads=sk, writes=[pkb])
                        T.op("pool", lambda g, ssl=ssl: g.memset(PT[ssl][64:128, 0, :, 0:64], 0.0), reads=[pka], writes=[pka])
                        T.op("pool", lambda g, ssl=ssl: g.memset(PT[ssl][64:128, 1, :, 128:192], 0.0), reads=[pkb], writes=[pkb])
                    mm_list = []
                    for kt2 in range(2):
                        kt = 2 * pr + kt2
                        for m in range(2):
                            for jq in range(TBT):
                                if diag and kt2 == 1 and jq == 0:
                                    continue
                                first = (pr == 0 and kt2 == 0)
                                last = diag and ((jq == 0 and kt2 == 0) or (jq == 1 and kt2 == 1))
                                mm_list.append((kt, kt2, m, jq, first, last))
                    for i, (kt, kt2, m, jq, first, last) in enumerate(mm_list):
                        T.op("pe", lambda pe, kt=kt, kt2=kt2, m=m, jq=jq, first=first, last=last, ssl=ssl, h=h: pe.matmul(
                            PS[:, 4 + 2 * m + jq, 0:129], lhsT=PT[ssl][:, kt2, m, jq * 128:(jq + 1) * 128],
                            rhs=VA[:, kt, h, 0:129], start=first, stop=last),
                            reads=[pka, pkb, "VA%d" % kt, "VAones"], writes=bkeys(4, 5, 6, 7), inc=(i == len(mm_list) - 1))
                for jq in range(TBT):
                    O0 = PS[:, 4 + jq, 0:128]
                    O1 = PS[:, 6 + jq, 0:128]
                    T.op("dve", lambda v, jq=jq: v.reciprocal(out=rl[:], in_=PS[:, 4 + jq:8:2, 128:129].rearrange("p b o -> p (b o)")),
                         reads=bkeys(4, 5, 6, 7), writes=["rl"])
                    T.op("dve", lambda v: v.tensor_tensor(out=rl[:, 1:2], in0=rl[:, 1:2], in1=neglam[:], op=ALU.mult),
                         reads=["rl", "neglam"], writes=["rl"])
                    T.op("dve", lambda v, O0=O0: v.tensor_scalar(out=o_t[:], in0=O0, scalar1=rl[:, 0:1], scalar2=None, op0=ALU.mult),
                         reads=bkeys(4, 5) + ("rl",), writes=["o_t"])
                    T.op("dve", lambda v, O1=O1: v.scalar_tensor_tensor(out=o_u[:], in0=O1, scalar=rl[:, 1:2], in1=o_t[:],
                                                                       op0=ALU.mult, op1=ALU.add),
                         reads=bkeys(6, 7) + ("rl", "o_t"), writes=["o_u"])
                    rms_rstd(o_u[:], ["o_u"], VD, SUBLN_EPS)
                    T.op("dve", lambda v: v.scalar_tensor_tensor(out=o_t[:], in0=o_u[:], scalar=rstd[:, 0:1], in1=gsb[:],
                                                                 op0=ALU.mult, op1=ALU.mult),
                         reads=["o_u", "rstd", "gsb"], writes=["o_t"])
                    T.op("dve", lambda v, jq=jq, h=h: v.tensor_tensor(out=ytm[:, jq, h, :], in0=o_t[:], in1=saz[:, jq, h * 128:(h + 1) * 128], op=ALU.mult),
                         reads=["o_t", "saz%d" % jq], writes=["ytm%d_%d" % (jq, h)])
            for j in range(TBT if DBG >= 4 else 0):
                pst = bank_bf(6).rearrange("p (k n) -> p k n", k=KD)
                for h in range(NH):
                    T.op("pe", lambda pe, h=h, j=j: pe.transpose(out=pst[:, h, :], in_=ytm[:, j, h, :], identity=identb[:]),
                         reads=["ytm%d_%d" % (j, h), "identb"], writes=bkeys(6), inc=(h == NH - 1))
                T.op("act", lambda a, j=j: a.copy(out=mixT[:, 4:8, j * 128:(j + 1) * 128], in_=pst[:, 0:NH, :]),
                     reads=bkeys(6), writes=["mixTa%d" % j])

            for j in range(TBT if DBG >= 5 else 0):
                t = b * TBT + j
                s = load_x(t)
                fk = "F%d" % s
                T.dma("sp", "pst_f", [(pst_f[:], p_d[t * 128:(t + 1) * 128, :])], writes=["pst_f"])
                for kk in range(KD):
                    for n in range(2):
                        T.op("pe", lambda pe, kk=kk, n=n, j=j: pe.matmul(PS[:, n, :], lhsT=mixT[:, kk, j * 128:(j + 1) * 128],
                                                                       rhs=WOUT[:, kk, n * 512:(n + 1) * 512],
                                                                       start=(kk == 0), stop=(kk == KD - 1)),
                             reads=["mixTc%d" % j, "mixTa%d" % j, "WOUT_%d_%d" % (kk, n)], writes=bkeys(n), inc=(kk == KD - 1))
                T.op("dve", lambda v, s=s: v.tensor_tensor(out=Fb[s][:], in0=Fb[s][:], in1=PS[:, 0:2, :].rearrange("p b n -> p (b n)"), op=ALU.add),
                     reads=[fk] + list(bkeys(0, 1)), writes=[fk])
                T.op("dve", lambda v: v.tensor_copy(out=pst_b[:], in_=pst_f[:]), reads=["pst_f"], writes=["pst_b"])
                pst7 = bank_bf(7).rearrange("p (k n) -> p k n", k=KD)
                for a2 in range(2):
                    T.op("pe", lambda pe, a2=a2: pe.transpose(out=pst7[:, a2, :], in_=pst_b[:, a2 * 128:(a2 + 1) * 128], identity=identb[:]),
                         reads=["pst_b", "identb"], writes=bkeys(7), inc=(a2 == 1))
                T.op("act", lambda a: a.copy(out=pT[:], in_=pst7[:, 0:2, :]), reads=bkeys(7), writes=["pT"])
                for kk in range(2):
                    for n in range(2):
                        T.op("pe", lambda pe, kk=kk, n=n: pe.matmul(PS[:, 4 + n, :], lhsT=pT[:, kk, :], rhs=WPLE[:, kk, n * 512:(n + 1) * 512],
                                                                  start=(kk == 0), stop=(kk == 1)),
                             reads=["pT", "WPLE_%d_%d" % (kk, n)], writes=bkeys(4 + n), inc=(kk == 1))
                rms_rstd(Fb[s][:], [fk], D, EPS)
                T.op("act", lambda a, s=s: a.activation(out=B1[:], in_=Fb[s][:], func=AF.Copy, scale=rstd[:, 0:1]),
                     reads=[fk, "rstd"], writes=["B1"])
                pst = bank_bf(6).rearrange("p (k n) -> p k n", k=KD)
                for kk in range(KD):
                    T.op("pe", lambda pe, kk=kk: pe.transpose(out=pst[:, kk, :], in_=B1[:, kk * 128:(kk + 1) * 128], identity=identb[:]),
                         reads=["B1", "identb"], writes=bkeys(6), inc=(kk == KD - 1))
                T.op("dve", lambda v: v.tensor_copy(out=uT[:], in_=pst), reads=bkeys(6), writes=["uT"])
                for kk in range(KD):
                    for n in range(2):
                        T.op("pe", lambda pe, kk=kk, n=n: pe.matmul(PS[:, 2 + n, :], lhsT=uT[:, kk, :], rhs=WGATE[:, kk, n * 512:(n + 1) * 512],
                                                                  start=(kk == 0), stop=(kk == KD - 1)),
                             reads=["uT", "WGATE_%d_%d" % (kk, n)], writes=bkeys(2 + n), inc=(kk == KD - 1))
                sigmoid_to(gate[:], ["gate", "cxs"], PS[:, 2:4, :].rearrange("p b n -> p (b n)"), bkeys(2, 3))
                T.op("dve", lambda v: v.tensor_tensor(out=gate[:], in0=gate[:], in1=PS[:, 4:6, :].rearrange("p b n -> p (b n)"), op=ALU.mult),
                     reads=["gate", "cxs"] + list(bkeys(4, 5)), writes=["gate", "cxs"])
                T.op("pool", lambda g, s=s: g.tensor_tensor(out=Fb[s][:], in0=Fb[s][:], in1=gate[:], op=ALU.add),
                     reads=[fk, "gate", "cxs"], writes=[fk])
                rms_rstd(Fb[s][:], [fk], D, EPS)
                T.op("dve", lambda v, s=s: v.scalar_tensor_tensor(out=Fb[s][:], in0=Fb[s][:], scalar=rstd[:, 0:1], in1=fnb[:],
                                                                 op0=ALU.mult, op1=ALU.mult),
                     reads=[fk, "rstd", "fnb"], writes=[fk])
                T.dma("pool", "out%d" % s, [(out_d[t * 128:(t + 1) * 128, :], Fb[s][:])], reads=[fk])

        for en in ("pool", "sp", "act", "dve", "pe"):
            T.wait_all(en)
    return nc


def make_in_maps(S, x, p, positions, norm_mix, w_in, conv_w, conv_b, lambda_q1, lambda_k1,
                 lambda_q2, lambda_k2, subln_g, w_out, norm_ple, w_ple_gate, w_ple_proj, final_norm):
    B = x.shape[0]
    NT = S // 128
    f = np.float32
    shared = {
        "gmix": np.ascontiguousarray(np.asarray(norm_mix[0], f).reshape(KD, 128).T),
        "gple": np.ascontiguousarray(np.asarray(norm_ple[0], f).reshape(KD, 128).T),
        "w_in": np.ascontiguousarray(np.asarray(w_in[0], f)),
        "convw": np.ascontiguousarray(np.asarray(conv_w[0], f).T.reshape(4, 128, 3).transpose(1, 0, 2)),
        "convb": np.ascontiguousarray(np.asarray(conv_b[0], f).reshape(4, 128).T),
        "lam4": np.ascontiguousarray(np.stack([np.asarray(lambda_q1[0], f), np.asarray(lambda_k1[0], f),
                                               np.asarray(lambda_q2[0], f), np.asarray(lambda_k2[0], f)])),
        "subg": np.ascontiguousarray(np.asarray(subln_g[0], f)),
        "w_out": np.ascontiguousarray(np.asarray(w_out[0], f)),
        "w_gate": np.ascontiguousarray(np.asarray(w_ple_gate[0], f)),
        "w_ple": np.ascontiguousarray(np.asarray(w_ple_proj[0], f)),
        "fnorm": np.ascontiguousarray(np.asarray(final_norm, f)),
        "ident": np.eye(128, dtype=f),
        "invf": (np.float32(500000.0) ** (-np.arange(8, dtype=f) / np.float32(8))).astype(f),
    }
    maps = []
    for b in range(B):
        m = dict(shared)
        m["x"] = np.ascontiguousarray(np.asarray(x[b], f))
        m["p"] = np.ascontiguousarray(np.asarray(p[0, b], f))
        m["pos_t"] = np.ascontiguousarray(np.asarray(positions[b], np.int32).reshape(NT, 128).T)
        maps.append(m)
    return maps


def kernel(**inputs):
    x = np.asarray(inputs["x"])
    B, S, _ = x.shape
    nc = build_nc(S)
    in_maps = make_in_maps(S, **{k: np.asarray(v) for k, v in inputs.items()})
    res = run_bass_kernel_spmd(nc, in_maps, core_ids=list(range(B)))
    out = np.stack([np.asarray(r["out"], dtype=np.float32) for r in res.results], axis=0)
    return out
```

```python
import math
import os
DBG = int(os.environ.get('KDBG', '99'))
SKIP = set(os.environ.get('KSKIP', '').split(','))
from contextlib import ExitStack

import numpy as np
import concourse.bass as bass
import concourse.mybir as mybir
from concourse.bass_utils import run_bass_kernel_spmd

F32 = mybir.dt.float32
BF16 = mybir.dt.bfloat16
I32 = mybir.dt.int32
AF = mybir.ActivationFunctionType
ALU = mybir.AluOpType

D = 1024
KD = D // 128
PLE = 256
CW = 512
NH = 4
VD = 128
TB = 256
TBT = TB // 128
EPS = 1e-6
SUBLN_EPS = 1e-5
LAM_INIT = 0.8 - 0.6 * math.exp(-0.3 * 0)
TWO_PI = 2.0 * math.pi


class Tracker:
    def __init__(self, nc, es):
        self.nc = nc
        self.es = es
        self.eng = {}
        for name, h in (("pe", nc.tensor), ("act", nc.scalar), ("dve", nc.vector),
                        ("pool", nc.gpsimd), ("sp", nc.sync)):
            sem = es.enter_context(nc.semaphore("sem_" + name))
            self.eng[name] = dict(h=h, sem=sem, key="E" + name, count=0, seen={}, pend=[])
        self.sems = {e["key"]: e["sem"] for e in self.eng.values()}
        self.res = {}
        self.dma_sems = {}

    def dma_sem(self, name):
        if name not in self.dma_sems:
            sem = self.es.enter_context(self.nc.semaphore("dsem_" + name))
            key = "D" + name
            self.sems[key] = sem
            self.dma_sems[name] = dict(sem=sem, key=key, count=0)
        return self.dma_sems[name]

    def _deps(self, e, reads, writes):
        deps = {}

        def add(tok):
            if tok is None:
                return
            k, v = tok
            if deps.get(k, 0) < v:
                deps[k] = v
        for r in reads:
            st = self.res.get(r)
            if st:
                add(st["w"])
        for w in writes:
            st = self.res.get(w)
            if st:
                add(st["w"])
                for k, v in st["r"].items():
                    add((k, v))
        for k, v in sorted(deps.items()):
            if e["key"] == "Epe" and k == "Epe":
                continue
            if e["seen"].get(k, 0) < v:
                e["h"].wait_ge(self.sems[k], v)
                e["seen"][k] = v

    def _commit(self, tok, reads, writes):
        k, v = tok
        for r in reads:
            st = self.res.setdefault(r, dict(w=None, r={}))
            if st["r"].get(k, 0) < v:
                st["r"][k] = v
        for w in writes:
            self.res[w] = dict(w=tok, r={})

    section = ""

    def op(self, engname, fn, reads=(), writes=(), inc=True, tag=""):
        if (self.section + ":" + engname) in SKIP or (self.section + ":" + tag) in SKIP:
            return None
        e = self.eng[engname]
        self._deps(e, reads, writes)
        ins = fn(e["h"])
        if inc:
            e["count"] += 1
            ins.then_inc(e["sem"], 1)
            tok = (e["key"], e["count"])
            self._commit(tok, reads, writes)
            for (r, w) in e["pend"]:
                self._commit(tok, r, w)
            e["pend"] = []
        else:
            e["pend"].append((tuple(reads), tuple(writes)))
        return ins

    def dma(self, queue, semname, pairs, reads=(), writes=(), **kw):
        e = self.eng[queue]
        ds = self.dma_sem(semname)
        self._deps(e, reads, writes)
        for (o, i) in pairs:
            e["h"].dma_start(out=o, in_=i, **kw).then_inc(ds["sem"], 16)
            ds["count"] += 16
        tok = (ds["key"], ds["count"])
        self._commit(tok, reads, writes)

    def wait_all(self, engname):
        e = self.eng[engname]
        for o in self.eng.values():
            if o["count"] > 0 and e["seen"].get(o["key"], 0) < o["count"]:
                e["h"].wait_ge(o["sem"], o["count"])
                e["seen"][o["key"]] = o["count"]
        for ds in self.dma_sems.values():
            if ds["count"] > 0 and e["seen"].get(ds["key"], 0) < ds["count"]:
                e["h"].wait_ge(ds["sem"], ds["count"])
                e["seen"][ds["key"]] = ds["count"]


def build_nc(S):
    NT = S // 128
    NB = S // TB
    nc = bass.Bass("TRN2", target_bir_lowering=False)

    def din(name, shape, dt=F32):
        return nc.dram_tensor(name, shape, dt, kind="ExternalInput").ap()

    x_d = din("x", [S, D])
    p_d = din("p", [S, PLE])
    pos_d = din("pos_t", [128, NT], I32)
    gmix_d = din("gmix", [128, KD])
    gple_d = din("gple", [128, KD])
    win_d = din("w_in", [D, 4096])
    convw_d = din("convw", [128, 4, 3])
    convb_d = din("convb", [128, 4])
    lam_d = din("lam4", [4, 64])
    subg_d = din("subg", [VD])
    wout_d = din("w_out", [D, D])
    wgate_d = din("w_gate", [D, D])
    wple_d = din("w_ple", [PLE, D])
    fnorm_d = din("fnorm", [D])
    ident_d = din("ident", [128, 128])
    invf_d = din("invf", [8])
    out_d = nc.dram_tensor("out", [S, D], F32, kind="ExternalOutput").ap()

    es = ExitStack()
    with es:
        T = Tracker(nc, es)

        def sb(name, shape, dt):
            return es.enter_context(nc.sbuf_tensor(name, shape, dt))

        KT = sb("KT", [128, NH, S], BF16)
        VA = sb("VA", [128, NT, NH, 130], BF16)
        W2 = sb("W2", [128, KD, 3072], BF16)
        WOUT = sb("WOUT", [128, KD, D], BF16)
        WGATE = sb("WGATE", [128, KD, D], BF16)
        WPLE = sb("WPLE", [128, 2, D], BF16)
        fnb = sb("fnb", [128, D], F32)
        identb = sb("identb", [128, 128], BF16)
        cosT = sb("cosT", [128, NT, 8], F32)
        sinT = sb("sinT", [128, NT, 8], F32)
        convw = sb("convw_s", [128, 4, 3], F32)
        convb = sb("convb_s", [128, 4], F32)
        gsb = sb("gsb", [128, VD], F32)
        neglam = sb("neglam", [128, 1], F32)
        gmix = sb("gmix_s", [128, KD], F32)
        gple = sb("gple_s", [128, KD], F32)
        ss = sb("ss", [128, 1], F32)
        lnv = sb("lnv", [128, 1], F32)
        rstd = sb("rstd", [128, 1], F32)
        Fb = [sb("F0", [128, D], F32), sb("F1", [128, D], F32)]
        B1 = sb("B1", [128, D], BF16)
        uTs = [sb("uT0", [128, KD, 128], BF16), sb("uT1", [128, KD, 128], BF16)]
        rtmp = sb("rtmp", [128, 4, 8, 8], F32)
        qk_tm = sb("qk_tm", [128, 512], BF16)

        PS = es.enter_context(nc.psum_tensor("PS", [128, 8, 512], F32))

        def bank_bf(b):
            return PS[:, b, :].bitcast(BF16)

        def bkeys(*bs):
            out = []
            for b in bs:
                out += ["ps7a", "ps7b"] if b == 7 else ["ps%d" % b]
            return tuple(out)

        cstate = dict(f=0)

        with ExitStack() as es0:
            def sb0(name, shape, dt):
                return es0.enter_context(nc.sbuf_tensor(name, shape, dt))
            WKV = sb0("WKV", [128, KD, 1024], BF16)
            stg = [sb0("stg0", [128, 1024], F32), sb0("stg1", [128, 1024], F32)]
            identf = sb0("identf", [128, 128], F32)
            posi = sb0("posi", [128, NT], I32)
            posf = sb0("posf", [128, NT], F32)
            invf = sb0("invf_s", [128, 8], F32)
            ang = sb0("ang", [128, NT, 8], F32)
            tq = sb0("tq", [128, NT, 8], F32)
            ki = sb0("ki", [128, NT, 8], I32)
            kf = sb0("kf", [128, NT, 8], F32)
            lamb = sb0("lamb", [128, 2, 2, 64], F32)
            lprod = sb0("lprod", [128, 2, 64], F32)
            lsum = sb0("lsum", [128, 2], F32)
            subg_s = sb0("subg_s", [128, VD], F32)

            small = "c_small"
            T.dma("sp", small, [(identf[:], ident_d[:, :])], writes=["identf"])
            T.dma("sp", small, [(posi[:], pos_d[:, :])], writes=["posi"])
            T.dma("sp", small, [(invf[:], invf_d.partition_broadcast(128))], writes=["invf"])
            T.dma("sp", small, [(gmix[:], gmix_d[:, :])], writes=["gmix"])
            T.dma("sp", small, [(gple[:], gple_d[:, :])], writes=["gple"])
            T.dma("sp", small, [(convw[:], convw_d[:, :, :])], writes=["convw"])
            T.dma("sp", small, [(convb[:], convb_d[:, :])], writes=["convb"])
            T.dma("sp", small, [(lamb[:].rearrange("p a b d -> p (a b d)"),
                                 lam_d.rearrange("a b -> (a b)").partition_broadcast(128))], writes=["lamb"])
            T.dma("sp", small, [(subg_s[:], subg_d.partition_broadcast(128))], writes=["subg_s"])
            T.dma("sp", small, [(fnb[:], fnorm_d.partition_broadcast(128))], writes=["fnb"])
            allsmall = ["identf", "posi", "invf", "gmix", "gple", "convw", "convb", "lamb", "subg_s", "fnb"]
            tok = T.res["fnb"]["w"]
            for r in allsmall:
                T.res[r]["w"] = tok

            T.op("dve", lambda v: v.tensor_copy(out=identb[:], in_=identf[:]), reads=["identf"], writes=["identb"])
            T.op("dve", lambda v: v.memset(VA[:, :, :, 128:129], 1.0), writes=["VAones"])
            T.op("dve", lambda v: v.tensor_scalar(out=gsb[:], in0=subg_s[:], scalar1=float(1.0 - LAM_INIT),
                                                  scalar2=None, op0=ALU.mult), reads=["subg_s"], writes=["gsb"])
            T.op("dve", lambda v: v.tensor_tensor(out=lprod[:], in0=lamb[:, :, 0, :], in1=lamb[:, :, 1, :], op=ALU.mult),
                 reads=["lamb"], writes=["lprod"])
            T.op("dve", lambda v: v.reduce_sum(out=lsum[:], in_=lprod[:], axis=mybir.AxisListType.X),
                 reads=["lprod"], writes=["lsum"])
            T.op("act", lambda a: a.activation(out=lsum[:], in_=lsum[:], func=AF.Exp), reads=["lsum"], writes=["lsum"])
            T.op("dve", lambda v: v.scalar_tensor_tensor(out=neglam[:], in0=lsum[:, 1:2], scalar=float(-LAM_INIT),
                                                         in1=lsum[:, 0:1], op0=ALU.add, op1=ALU.subtract),
                 reads=["lsum"], writes=["neglam"])

            T.op("dve", lambda v: v.tensor_copy(out=posf[:], in_=posi[:]), reads=["posi"], writes=["posf"])
            T.op("dve", lambda v: v.tensor_tensor(out=ang[:], in0=posf[:].unsqueeze(2).broadcast_to([128, NT, 8]),
                                                  in1=invf[:].unsqueeze(1).broadcast_to([128, NT, 8]), op=ALU.mult),
                 reads=["posf", "invf"], writes=["ang"])
            for (dst, shift, nm) in ((sinT, 0.0, "sinT"), (cosT, 0.25, "cosT")):
                T.op("dve", lambda v: v.tensor_scalar(out=tq[:], in0=ang[:], scalar1=float(1.0 / TWO_PI), scalar2=float(shift),
                                                      op0=ALU.mult, op1=ALU.add), reads=["ang"], writes=["tq"])
                T.op("dve", lambda v: v.tensor_copy(out=ki[:], in_=tq[:]), reads=["tq"], writes=["ki"])
                T.op("dve", lambda v: v.tensor_copy(out=kf[:], in_=ki[:]), reads=["ki"], writes=["kf"])
                T.op("dve", lambda v: v.tensor_tensor(out=tq[:], in0=tq[:], in1=kf[:], op=ALU.subtract),
                     reads=["tq", "kf"], writes=["tq"])
                T.op("dve", lambda v: v.tensor_single_scalar(out=kf[:], in_=tq[:], scalar=0.5, op=ALU.is_gt),
                     reads=["tq"], writes=["kf"])
                T.op("dve", lambda v: v.tensor_tensor(out=tq[:], in0=tq[:], in1=kf[:], op=ALU.subtract),
                     reads=["tq", "kf"], writes=["tq"])
                T.op("dve", lambda v: v.tensor_single_scalar(out=kf[:], in_=tq[:], scalar=-0.5, op=ALU.is_lt),
                     reads=["tq"], writes=["kf"])
                T.op("dve", lambda v: v.tensor_tensor(out=tq[:], in0=tq[:], in1=kf[:], op=ALU.add),
                     reads=["tq", "kf"], writes=["tq"])
                T.op("act", lambda a, dst=dst: a.activation(out=dst[:], in_=tq[:], func=AF.Sin, scale=6.283185),
                     reads=["tq"], writes=[nm])

            wstate = dict(i=0, chunk=0)

            def cvt(en, dst, src, scale_ap, reads, writes):
                if scale_ap is None:
                    if en == "act":
                        T.op("act", lambda a: a.copy(out=dst, in_=src), reads=reads, writes=writes)
                    else:
                        T.op(en, lambda v: v.tensor_copy(out=dst, in_=src), reads=reads, writes=writes)
                else:
                    if en == "act":
                        T.op("act", lambda a: a.activation(out=dst, in_=src, func=AF.Copy, scale=scale_ap),
                             reads=reads + ["gmix", "gple"], writes=writes)
                    else:
                        T.op(en, lambda v: v.tensor_scalar(out=dst, in0=src, scalar1=scale_ap, scalar2=None, op0=ALU.mult),
                             reads=reads + ["gmix", "gple"], writes=writes)

            def emit_chunk(queue, engs, srcs, outs):
                c = wstate["chunk"]
                wstate["chunk"] += 1
                sl = c % 2
                key = "stg%d" % sl
                T.dma(queue, key + queue, [(stg[sl][:, c0:c0 + n], d) for (c0, n, d) in srcs], writes=[key])
                pending = []
                for (dst, c0, scale, wkey) in outs:
                    en = engs[wstate["i"] % len(engs)]
                    wstate["i"] += 1
                    pending.append((en, dst, stg[sl][:, c0:c0 + 512], scale, [key], [wkey]))
                return pending

            def run_cvts(pending):
                for (en, dst, src, scale, r, w) in pending:
                    cvt(en, dst, src, scale, r, w)

            prev = None
            for kk in range(KD):
                rows = slice(kk * 128, (kk + 1) * 128)
                cur = emit_chunk("sp", ["dve", "act"], [(0, 1024, win_d[rows, 2560:3584])],
                                 [(WKV[:, kk, 0:512], 0, gmix[:, kk:kk + 1], "WKV_%d_0" % kk),
                                  (WKV[:, kk, 512:1024], 512, gmix[:, kk:kk + 1], "WKV_%d_1" % kk)])
                if prev is not None:
                    run_cvts(prev)
                prev = cur
            run_cvts(prev)

            rest = []
            for kk in range(KD):
                rows = slice(kk * 128, (kk + 1) * 128)
                gs = gmix[:, kk:kk + 1]
                rest.append(([(0, 1024, win_d[rows, 0:1024])],
                             [(W2[:, kk, 1024:1536], 0, gs, "W2_%d_2" % kk), (W2[:, kk, 1536:2048], 512, gs, "W2_%d_3" % kk)]))
                rest.append(([(0, 1024, win_d[rows, 1024:2048])],
                             [(W2[:, kk, 2048:2560], 0, gs, "W2_%d_4" % kk), (W2[:, kk, 2560:3072], 512, gs, "W2_%d_5" % kk)]))
                rest.append(([(0, 512, win_d[rows, 2048:2560]), (512, 512, win_d[rows, 3584:4096])],
                             [(W2[:, kk, 0:512], 0, gs, "W2_%d_0" % kk), (W2[:, kk, 512:1024], 512, gs, "W2_%d_1" % kk)]))
            for (Wd, Wsb, gcol, nm, nk) in ((wout_d, WOUT, None, "WOUT", KD), (wgate_d, WGATE, gple, "WGATE", KD), (wple_d, WPLE, None, "WPLE", 2)):
                for kk in range(nk):
                    sc = None if gcol is None else gcol[:, kk:kk + 1]
                    rest.append(([(0, 1024, Wd[kk * 128:(kk + 1) * 128, :])],
                                 [(Wsb[:, kk, 0:512], 0, sc, "%s_%d_0" % (nm, kk)), (Wsb[:, kk, 512:1024], 512, sc, "%s_%d_1" % (nm, kk))]))
            rstate = dict(prev=None)

            def pump_rest(n):
                for _ in range(n):
                    if rest:
                        srcs, outs = rest.pop(0)
                        cur = emit_chunk("pool", ["act", "dve"], srcs, outs)
                    else:
                        cur = None
                    if rstate["prev"] is not None:
                        run_cvts(rstate["prev"])
                    rstate["prev"] = cur

            def load_x(t):
                s = cstate["f"] % 2
                cstate["f"] += 1
                T.dma("sp", "F%d" % s, [(Fb[s][:], x_d[t * 128:(t + 1) * 128, :])], writes=["F%d" % s])
                return s

            def rms_rstd(src_ap, src_keys, n, eps):
                T.op("act", lambda a: a.activation(out=B1[:, 0:n], in_=src_ap, func=AF.Square, accum_out=ss[:]),
                     reads=src_keys, writes=["B1", "ss"])
                T.op("act", lambda a: a.activation(out=lnv[:], in_=ss[:], func=AF.Ln, scale=1.0 / n, bias=float(eps)),
                     reads=["ss"], writes=["lnv"])
                T.op("act", lambda a: a.activation(out=rstd[:], in_=lnv[:], func=AF.Exp, scale=-0.5),
                     reads=["lnv"], writes=["rstd"])

            def norm_transpose(s, par):
                fk = "F%d" % s
                rms_rstd(Fb[s][:], [fk], D, EPS)
                T.op("dve", lambda v: v.tensor_scalar(out=B1[:], in0=Fb[s][:], scalar1=rstd[:, 0:1], scalar2=None, op0=ALU.mult),
                     reads=[fk, "rstd"], writes=["B1"])
                pst = bank_bf(6).rearrange("p (k n) -> p k n", k=KD)
                for kk in range(KD):
                    T.op("pe", lambda pe, kk=kk: pe.transpose(out=pst[:, kk, :], in_=B1[:, kk * 128:(kk + 1) * 128], identity=identb[:]),
                         reads=["B1", "identb"], writes=bkeys(6), inc=(kk == KD - 1))
                T.op("act", lambda a: a.copy(out=uTs[par][:], in_=pst), reads=bkeys(6), writes=["uT%d" % par])

            def rope_to(psrc, t, dst_key):
                src3 = psrc.rearrange("p (g d) -> p g d", g=8)
                dst3 = qk_tm[:].rearrange("p (g d) -> p g d", g=8)
                cb_ = cosT[:, t, :].unsqueeze(1).broadcast_to([128, 8, 8])
                sb_ = sinT[:, t, :].unsqueeze(1).broadcast_to([128, 8, 8])
                x1 = src3[:, :, 0:8]
                x2 = src3[:, :, 8:16]
                T.op("dve", lambda v: v.tensor_copy(out=dst3[:, :, 16:64], in_=src3[:, :, 16:64]), reads=[dst_key], writes=["qk_rest"], tag="rope")
                T.op("dve", lambda v: v.tensor_tensor(out=rtmp[:, 0], in0=x1, in1=cb_, op=ALU.mult), reads=[dst_key, "cosT"], writes=["rt0"], tag="rope")
                T.op("dve", lambda v: v.tensor_tensor(out=rtmp[:, 1], in0=x2, in1=sb_, op=ALU.mult), reads=[dst_key, "sinT"], writes=["rt1"], tag="rope")
                T.op("dve", lambda v: v.tensor_tensor(out=rtmp[:, 2], in0=x2, in1=cb_, op=ALU.mult), reads=[dst_key, "cosT"], writes=["rt2"], tag="rope")
                T.op("dve", lambda v: v.tensor_tensor(out=rtmp[:, 3], in0=x1, in1=sb_, op=ALU.mult), reads=[dst_key, "sinT"], writes=["rt3"], tag="rope")
                T.op("dve", lambda v: v.tensor_tensor(out=dst3[:, :, 0:8], in0=rtmp[:, 0], in1=rtmp[:, 1], op=ALU.subtract),
                     reads=["rt0", "rt1"], writes=["qk_r1"], tag="rope")
                T.op("dve", lambda v: v.tensor_tensor(out=dst3[:, :, 8:16], in0=rtmp[:, 2], in1=rtmp[:, 3], op=ALU.add),
                     reads=["rt2", "rt3"], writes=["qk_r2"], tag="rope")

            QK_KEYS = ["qk_rest", "qk_r1", "qk_r2"]

            def p1_matmuls(par):
                for kk in range(KD):
                    for g in range(2):
                        T.op("pe", lambda pe, kk=kk, g=g: pe.matmul(PS[:, g, :], lhsT=uTs[par][:, kk, :], rhs=WKV[:, kk, g * 512:(g + 1) * 512],
                                                                  start=(kk == 0), stop=(kk == KD - 1)),
                             reads=["uT%d" % par, "WKV_%d_%d" % (kk, g)], writes=bkeys(g), inc=(kk == KD - 1))

            def p1_post(t):
                T.op("act", lambda a: a.copy(out=VA[:, t, :, 0:128], in_=PS[:, 1, :].rearrange("p (h d) -> p h d", h=NH)),
                     reads=bkeys(1), writes=["VA%d" % t])
                rope_to(PS[:, 0, :], t, "ps0")
                pst = bank_bf(7).rearrange("p (k n) -> p k n", k=KD)
                for h in range(NH):
                    T.op("pe", lambda pe, h=h: pe.transpose(out=pst[:, h, :], in_=qk_tm[:, h * 128:(h + 1) * 128], identity=identb[:]),
                         reads=QK_KEYS + ["identb"], writes=bkeys(7), inc=(h == NH - 1))
                T.op("dve", lambda v: v.tensor_copy(out=KT[:, :, t * 128:(t + 1) * 128], in_=pst[:, 0:NH, :]),
                     reads=bkeys(7), writes=["KT%d" % t])

            nt1 = NT if DBG >= 1 else 0
            s_next = None
            if nt1:
                s0_ = load_x(0)
                norm_transpose(s0_, 0)
                if nt1 > 1:
                    s_next = load_x(1)
            for t in range(nt1):
                p1_matmuls(t % 2)
                if t + 1 < nt1:
                    s_cur = s_next
                    if t + 2 < nt1:
                        s_next = load_x(t + 2)
                    norm_transpose(s_cur, (t + 1) % 2)
                p1_post(t)
                pump_rest(2)
            while rest or rstate["prev"] is not None:
                pump_rest(1)

            for en in ("pe", "act", "dve", "pool", "sp"):
                T.wait_all(en)

        saz = sb("saz", [128, TBT, 512], BF16)
        QTz = sb("QTz", [128, NH, 2, TB], BF16)
        mixT = sb("mixT", [128, KD, TB], BF16)
        PT = [sb("PT0", [128, 2, 2, TB], BF16), sb("PT1", [128, 2, 2, TB], BF16)]
        sz = sb("sz", [128, 512], BF16)
        p_tm = sb("p_tm", [128, 512], BF16)
        g_tm = sb("g_tm", [128, 512], BF16)
        Pbuf = sb("Pbuf", [128, 4, 130], F32)
        acc = sb("acc", [128, 4, 128], F32)
        ytm = sb("ytm", [128, TBT, NH, VD], BF16)
        rl = sb("rl", [128, 4], F32)
        Osb = [sb("Osb0", [128, 4, 129], F32), sb("Osb1", [128, 4, 129], F32)]
        ctmp = sb("ctmp", [128, 4, 128], F32)
        o_u = ctmp[:, 0:2, :]
        o_sq = ctmp[:, 2:4, :]
        ssq = sb("ssq", [128, TBT], F32)
        lnv2 = sb("lnv2", [128, TBT], F32)
        rs2 = sb("rs2", [128, TBT], F32)
        gate = sb("gate", [128, D], F32)
        sg = gate[:, 0:512]
        cxs = gate[:, 512:1024]
        pst_f = sb("pst_f", [128, PLE], F32)
        pst_b = sb("pst_b", [128, PLE], BF16)
        pTs = [sb("pT0", [128, 2, 128], BF16), sb("pT1", [128, 2, 128], BF16)]

        print("SBUF bytes remaining at pass-2 peak:", nc.sbuf_bytes_remaining)
        T.op("pool", lambda g: g.memset(QTz[:], 0.0), writes=["QTz_lo", "QTz_hi"])
        T.op("pool", lambda g: g.memset(Pbuf[:], 0.0), writes=["Pbuf_hist", "Pbuf_main"])

        def sigmoid_to(dst_ap, dst_keys, src_ap, src_keys, tag="sig"):
            T.op("act", lambda a: a.activation(out=dst_ap, in_=src_ap, func=AF.Exp, scale=-1.0), reads=src_keys, writes=dst_keys, tag=tag)
            T.op("act", lambda a: a.activation(out=dst_ap, in_=dst_ap, func=AF.Ln, bias=1.0), reads=dst_keys, writes=dst_keys, tag=tag)
            T.op("act", lambda a: a.activation(out=dst_ap, in_=dst_ap, func=AF.Exp, scale=-1.0), reads=dst_keys, writes=dst_keys, tag=tag)

        def w2keys(g):
            return ["W2_%d_%d" % (kk, g) for kk in range(KD)]

        pst7 = bank_bf(7).rearrange("p (k n) -> p k n", k=KD)
        GA = (0, 1, 5)
        GB = (2, 3, 4)

        def p2_matmuls(par, groups):
            for kk in range(KD):
                for g in groups:
                    T.op("pe", lambda pe, kk=kk, g=g: pe.matmul(PS[:, g, :], lhsT=uTs[par][:, kk, :], rhs=W2[:, kk, g * 512:(g + 1) * 512],
                                                              start=(kk == 0), stop=(kk == KD - 1)),
                         reads=["uT%d" % par, "W2_%d_%d" % (kk, g)], writes=bkeys(g), inc=(kk == KD - 1))

        def p2_postA(t, j):
            sigmoid_to(sg, ["gate"], PS[:, 1, :], bkeys(1))
            T.op("dve", lambda v: v.tensor_tensor(out=saz[:, j, :], in0=PS[:, 1, :], in1=sg, op=ALU.mult),
                 reads=bkeys(1) + ("gate",), writes=["saz%d" % j])
            sigmoid_to(sg, ["gate"], PS[:, 5, :], bkeys(5))
            T.op("dve", lambda v: v.tensor_tensor(out=sz[:], in0=PS[:, 5, :], in1=sg, op=ALU.mult),
                 reads=bkeys(5) + ("gate",), writes=["sz"])
            rope_to(PS[:, 0, :], t, "ps0")
            for h in range(NH):
                T.op("pe", lambda pe, h=h: pe.transpose(out=pst7[:, h, :], in_=qk_tm[:, h * 128:(h + 1) * 128], identity=identb[:]),
                     reads=QK_KEYS + ["identb"], writes=bkeys(7), inc=(h == NH - 1))
            T.op("dve", lambda v: v.tensor_copy(out=QTz[0:64, :, 0, j * 128:(j + 1) * 128], in_=pst7[0:64, 0:NH, :]),
                 reads=bkeys(7), writes=["QTz_lo"])
            T.op("act", lambda a: a.copy(out=QTz[64:128, :, 1, j * 128:(j + 1) * 128], in_=pst7[64:128, 0:NH, :]),
                 reads=bkeys(7), writes=["QTz_hi"])

        def p2_postB1(t, j):
            T.op("act", lambda a: a.copy(out=cxs, in_=PS[:, 2, :]), reads=bkeys(2), writes=["cxs"])
            T.op("dve", lambda v: v.tensor_tensor(out=p_tm[:], in0=PS[:, 4, :], in1=cxs, op=ALU.mult),
                 reads=bkeys(4) + ("cxs",), writes=["p_tm"])
            T.op("dve", lambda v: v.tensor_tensor(out=g_tm[:], in0=PS[:, 3, :], in1=sz[:], op=ALU.mult),
                 reads=bkeys(3) + ("sz",), writes=["g_tm"])

        def p2_postB2(t, j):
            for c in range(4):
                T.op("pe", lambda pe, c=c: pe.transpose(out=pst7[:, 4 + c, :], in_=p_tm[:, c * 128:(c + 1) * 128], identity=identb[:]),
                     reads=["p_tm", "identb"], writes=bkeys(7), inc=False)
            for c in range(4):
                T.op("pe", lambda pe, c=c: pe.transpose(out=pst7[:, c, :], in_=g_tm[:, c * 128:(c + 1) * 128], identity=identb[:]),
                     reads=["g_tm", "identb"], writes=bkeys(7), inc=(c == 3))
            T.op("dve", lambda v: v.tensor_copy(out=Pbuf[:, :, 2:130], in_=pst7[:, 4:8, :]), reads=bkeys(7), writes=["Pbuf_main"])
            gT = g_tm[:].rearrange("p (c n) -> p c n", c=4)
            T.op("dve", lambda v: v.tensor_copy(out=gT, in_=pst7[:, 0:4, :]), reads=bkeys(7), writes=["g_tm"])

            def wb(k):
                return convw[:, :, k:k + 1].broadcast_to([128, 4, 128])
            T.op("dve", lambda v: v.tensor_tensor(out=acc[:], in0=Pbuf[:, :, 2:130], in1=wb(2), op=ALU.mult),
                 reads=["Pbuf_main", "convw"], writes=["acc"])
            T.op("dve", lambda v: v.tensor_tensor(out=ctmp[:], in0=Pbuf[:, :, 1:129], in1=wb(1), op=ALU.mult),
                 reads=["Pbuf_main", "Pbuf_hist", "convw"], writes=["o_u", "o_sq"])
            T.op("dve", lambda v: v.tensor_tensor(out=acc[:], in0=acc[:], in1=ctmp[:], op=ALU.add),
                 reads=["acc", "o_u", "o_sq"], writes=["acc"])
            T.op("dve", lambda v: v.tensor_tensor(out=ctmp[:], in0=Pbuf[:, :, 0:128], in1=wb(0), op=ALU.mult),
                 reads=["Pbuf_main", "Pbuf_hist", "convw"], writes=["o_u", "o_sq"])
            T.op("dve", lambda v: v.tensor_tensor(out=acc[:], in0=acc[:], in1=ctmp[:], op=ALU.add),
                 reads=["acc", "o_u", "o_sq"], writes=["acc"])
            T.op("dve", lambda v: v.tensor_tensor(out=acc[:], in0=acc[:], in1=convb[:].unsqueeze(2).broadcast_to([128, 4, 128]), op=ALU.add),
                 reads=["acc", "convb"], writes=["acc"])
            T.op("dve", lambda v: v.tensor_tensor(out=mixT[:, 0:4, j * 128:(j + 1) * 128], in0=acc[:], in1=gT, op=ALU.mult),
                 reads=["acc", "g_tm"], writes=["mixTc%d" % j])
            T.op("dve", lambda v: v.tensor_copy(out=Pbuf[:, :, 0:2], in_=Pbuf[:, :, 128:130]),
                 reads=["Pbuf_main"], writes=["Pbuf_hist"])

        for b in range(NB if DBG >= 2 else 0):
            t0, t1 = b * TBT, b * TBT + 1
            sx0 = load_x(t0)
            sx1 = load_x(t1)
            norm_transpose(sx0, 0)
            p2_matmuls(0, GA)
            norm_transpose(sx1, 1)
            p2_matmuls(0, GB)
            p2_postA(t0, 0)
            p2_matmuls(1, GA)
            p2_postB1(t0, 0)
            p2_postB2(t0, 0)
            p2_matmuls(1, GB)
            p2_postA(t1, 1)
            p2_postB1(t1, 1)

            T.section = ""
            npairs = b + 1
            steps = [(h, pr) for h in range(NH if DBG >= 3 else 0) for pr in range(npairs)]

            def emit_qk(i):
                h, pr = steps[i]
                diag = (pr == npairs - 1)
                ssl = i % 2
                psS = PS[:, 2 * ssl:2 * ssl + 2, :].rearrange("p b (m q) -> p b m q", m=2)
                sk = bkeys(2 * ssl, 2 * ssl + 1)
                n_mm = 0
                for kt2 in range(2):
                    kt = 2 * pr + kt2
                    q0 = 128 if (diag and kt2 == 1) else 0
                    for m in range(2):
                        n_mm += 1
                        T.op("pe", lambda pe, kt=kt, kt2=kt2, m=m, q0=q0, h=h, psS=psS: pe.matmul(
                            psS[:, kt2, m, q0:TB], lhsT=KT[:, h, kt * 128:(kt + 1) * 128], rhs=QTz[:, h, m, q0:TB],
                            start=True, stop=True),
                            reads=["KT%d" % kt, "QTz_lo", "QTz_hi"], writes=sk, inc=(n_mm == 4))

            def emit_exp(i):
                h, pr = steps[i]
                diag = (pr == npairs - 1)
                ssl = i % 2
                psS = PS[:, 2 * ssl:2 * ssl + 2, :].rearrange("p b (m q) -> p b m q", m=2)
                sk = bkeys(2 * ssl, 2 * ssl + 1)
                pka, pkb = "PT%da" % ssl, "PT%db" % ssl
                if not diag:
                    T.op("act", lambda a: a.activation(out=PT[ssl][:], in_=psS, func=AF.Exp, scale=0.125),
                         reads=sk, writes=[pka, pkb])
                else:
                    T.op("act", lambda a: a.activation(out=PT[ssl][:, 0], in_=psS[:, 0], func=AF.Exp, scale=0.125),
                         reads=sk, writes=[pka])
                    T.op("act", lambda a: a.activation(out=PT[ssl][:, 1, :, 128:TB], in_=psS[:, 1, :, 128:TB],
                                                       func=AF.Exp, scale=0.125),
                         reads=sk, writes=[pkb])
                    T.op("pool", lambda g: g.memset(PT[ssl][64:128, 0, :, 0:64], 0.0), reads=[pka], writes=[pka])
                    T.op("pool", lambda g: g.memset(PT[ssl][64:128, 1, :, 128:192], 0.0), reads=[pkb], writes=[pkb])

            def emit_pv(i):
                h, pr = steps[i]
                diag = (pr == npairs - 1)
                ssl = i % 2
                pka, pkb = "PT%da" % ssl, "PT%db" % ssl
                mm_list = []
                for kt2 in range(2):
                    kt = 2 * pr + kt2
                    for m in range(2):
                        for jq in range(TBT):
                            if diag and kt2 == 1 and jq == 0:
                                continue
                            first = (pr == 0 and kt2 == 0)
                            last = diag and ((jq == 0 and kt2 == 0) or (jq == 1 and kt2 == 1))
                            mm_list.append((kt, kt2, m, jq, first, last))
                for ii, (kt, kt2, m, jq, first, last) in enumerate(mm_list):
                    T.op("pe", lambda pe, kt=kt, kt2=kt2, m=m, jq=jq, first=first, last=last, h=h: pe.matmul(
                        PS[:, 4 + 2 * m + jq, 0:129], lhsT=PT[ssl][:, kt2, m, jq * 128:(jq + 1) * 128],
                        rhs=VA[:, kt, h, 0:129], start=first, stop=last),
                        reads=[pka, pkb, "VA%d" % kt, "VAones"], writes=bkeys(4, 5, 6, 7), inc=(ii == len(mm_list) - 1))

            def norm_stage1(h):
                osl = h % 2
                Ob = Osb[osl]
                ok = "Osb%d" % osl
                T.op("dve", lambda v: v.tensor_copy(out=Ob[:], in_=PS[:, 4:8, 0:129]), reads=bkeys(4, 5, 6, 7), writes=[ok])
                T.op("dve", lambda v: v.reciprocal(out=rl[:], in_=Ob[:, :, 128:129].rearrange("p a o -> p (a o)")),
                     reads=[ok], writes=["rl"])
                T.op("dve", lambda v: v.tensor_scalar(out=rl[:, 2:4], in0=rl[:, 2:4], scalar1=neglam[:, 0:1], scalar2=None, op0=ALU.mult),
                     reads=["rl", "neglam"], writes=["rl"])
                T.op("dve", lambda v: v.tensor_tensor(out=Ob[:, :, 0:128], in0=Ob[:, :, 0:128],
                                                      in1=rl[:].unsqueeze(2).broadcast_to([128, 4, 128]), op=ALU.mult),
                     reads=[ok, "rl"], writes=[ok])
                T.op("dve", lambda v: v.tensor_tensor(out=o_u, in0=Ob[:, 0:2, 0:128], in1=Ob[:, 2:4, 0:128], op=ALU.add),
                     reads=[ok], writes=["o_u"])
                T.op("dve", lambda v: v.tensor_tensor(out=o_sq, in0=o_u, in1=o_u, op=ALU.mult),
                     reads=["o_u"], writes=["o_sq"])
                T.op("dve", lambda v: v.reduce_sum(out=ssq[:], in_=o_sq, axis=mybir.AxisListType.X),
                     reads=["o_sq"], writes=["ssq"])

            def norm_stage2(h):
                T.op("act", lambda a: a.activation(out=lnv2[:], in_=ssq[:], func=AF.Ln, scale=1.0 / VD, bias=float(SUBLN_EPS)),
                     reads=["ssq"], writes=["lnv2"])
                T.op("act", lambda a: a.activation(out=rs2[:], in_=lnv2[:], func=AF.Exp, scale=-0.5),
                     reads=["lnv2"], writes=["rs2"])
                T.op("dve", lambda v: v.tensor_tensor(out=o_sq, in0=o_u, in1=rs2[:].unsqueeze(2).broadcast_to([128, TBT, 128]), op=ALU.mult),
                     reads=["o_u", "rs2"], writes=["o_sq"])
                T.op("dve", lambda v: v.tensor_tensor(out=o_sq, in0=o_sq, in1=gsb[:].unsqueeze(1).broadcast_to([128, TBT, 128]), op=ALU.mult),
                     reads=["o_sq", "gsb"], writes=["o_sq"])
                T.op("dve", lambda v: v.tensor_tensor(out=ytm[:, :, h, :], in0=o_sq, in1=saz[:, :, h * 128:(h + 1) * 128], op=ALU.mult),
                     reads=["o_sq"] + ["saz%d" % j for j in range(TBT)], writes=["ytm%d_%d" % (j, h) for j in range(TBT)])

            deferred = []
            if steps:
                emit_qk(0)
            for i, (h, pr) in enumerate(steps):
                if i + 1 < len(steps):
                    emit_qk(i + 1)
                emit_exp(i)
                emit_pv(i)
                if pr == npairs - 1:
                    while deferred:
                        norm_stage2(deferred.pop(0)[1])
                    norm_stage1(h)
                    deferred.append((i + 2, h))
                while deferred and (deferred[0][0] <= i or i == len(steps) - 1):
                    norm_stage2(deferred.pop(0)[1])
            if DBG >= 2:
                p2_postB2(b * TBT + 1, 1)
            for j in range(TBT if DBG >= 4 else 0):
                pst = bank_bf(6).rearrange("p (k n) -> p k n", k=KD)
                for h in range(NH):
                    T.op("pe", lambda pe, h=h, j=j: pe.transpose(out=pst[:, h, :], in_=ytm[:, j, h, :], identity=identb[:]),
                         reads=["ytm%d_%d" % (j, h), "identb"], writes=bkeys(6), inc=(h == NH - 1))
                T.op("act", lambda a, j=j: a.copy(out=mixT[:, 4:8, j * 128:(j + 1) * 128], in_=pst[:, 0:NH, :]),
                     reads=bkeys(6), writes=["mixTa%d" % j])

            if DBG >= 5:
                tt = [b * TBT + j for j in range(TBT)]
                sx = [load_x(t) for t in tt]
                fks = ["F%d" % s_ for s_ in sx]
                opb = [(0, 1), (2, 3)]
                gb = [(4, 5), (0, 1)]
                plb = [(2, 3), (6, 7)]

                def ps2(bb):
                    return PS[:, bb[0]:bb[0] + 2, :].rearrange("p b n -> p (b n)")
                pst = bank_bf(6).rearrange("p (k n) -> p k n", k=KD)
                for j in range(TBT):
                    t = tt[j]
                    T.dma("sp", "pst_f", [(pst_f[:], p_d[t * 128:(t + 1) * 128, :])], writes=["pst_f"])
                    T.op("dve", lambda v: v.tensor_copy(out=pst_b[:], in_=pst_f[:]), reads=["pst_f"], writes=["pst_b"])
                    for a2 in range(2):
                        T.op("pe", lambda pe, a2=a2: pe.transpose(out=pst7[:, a2, :], in_=pst_b[:, a2 * 128:(a2 + 1) * 128], identity=identb[:]),
                             reads=["pst_b", "identb"], writes=bkeys(7), inc=(a2 == 1))
                    T.op("act", lambda a, j=j: a.copy(out=pTs[j][:], in_=pst7[:, 0:2, :]), reads=bkeys(7), writes=["pT%d" % j])
                for j in range(TBT):
                    for kk in range(KD):
                        for n in range(2):
                            T.op("pe", lambda pe, kk=kk, n=n, j=j: pe.matmul(PS[:, opb[j][n], :], lhsT=mixT[:, kk, j * 128:(j + 1) * 128],
                                                                           rhs=WOUT[:, kk, n * 512:(n + 1) * 512],
                                                                           start=(kk == 0), stop=(kk == KD - 1)),
                                 reads=["mixTc%d" % j, "mixTa%d" % j, "WOUT_%d_%d" % (kk, n)], writes=bkeys(opb[j][n]), inc=(kk == KD - 1))
                for j in range(TBT):
                    T.op("dve", lambda v, j=j: v.tensor_tensor(out=Fb[sx[j]][:], in0=Fb[sx[j]][:], in1=ps2(opb[j]), op=ALU.add),
                         reads=[fks[j]] + list(bkeys(*opb[j])), writes=[fks[j]])
                    rms_rstd(Fb[sx[j]][:], [fks[j]], D, EPS)
                    T.op("dve", lambda v, j=j: v.tensor_scalar(out=B1[:], in0=Fb[sx[j]][:], scalar1=rstd[:, 0:1], scalar2=None, op0=ALU.mult),
                         reads=[fks[j], "rstd"], writes=["B1"])
                    for kk in range(KD):
                        T.op("pe", lambda pe, kk=kk: pe.transpose(out=pst[:, kk, :], in_=B1[:, kk * 128:(kk + 1) * 128], identity=identb[:]),
                             reads=["B1", "identb"], writes=bkeys(6), inc=(kk == KD - 1))
                    T.op("act", lambda a, j=j: a.copy(out=uTs[j][:], in_=pst), reads=bkeys(6), writes=["uT%d" % j])
                for j in range(TBT):
                    for kk in range(2):
                        for n in range(2):
                            T.op("pe", lambda pe, kk=kk, n=n, j=j: pe.matmul(PS[:, plb[j][n], :], lhsT=pTs[j][:, kk, :], rhs=WPLE[:, kk, n * 512:(n + 1) * 512],
                                                                           start=(kk == 0), stop=(kk == 1)),
                                 reads=["pT%d" % j, "WPLE_%d_%d" % (kk, n)], writes=bkeys(plb[j][n]), inc=(kk == 1))
                for j in range(TBT):
                    for kk in range(KD):
                        for n in range(2):
                            T.op("pe", lambda pe, kk=kk, n=n, j=j: pe.matmul(PS[:, gb[j][n], :], lhsT=uTs[j][:, kk, :], rhs=WGATE[:, kk, n * 512:(n + 1) * 512],
                                                                           start=(kk == 0), stop=(kk == KD - 1)),
                                 reads=["uT%d" % j, "WGATE_%d_%d" % (kk, n)], writes=bkeys(gb[j][n]), inc=(kk == KD - 1))
                for j in range(TBT):
                    t = tt[j]
                    s_ = sx[j]
                    fk = fks[j]
                    sigmoid_to(gate[:], ["gate", "cxs"], ps2(gb[j]), bkeys(*gb[j]))
                    T.op("dve", lambda v, j=j: v.tensor_tensor(out=gate[:], in0=gate[:], in1=ps2(plb[j]), op=ALU.mult),
                         reads=["gate", "cxs"] + list(bkeys(*plb[j])), writes=["gate", "cxs"])
                    T.op("dve", lambda v, s_=s_: v.tensor_tensor(out=Fb[s_][:], in0=Fb[s_][:], in1=gate[:], op=ALU.add),
                         reads=[fk, "gate", "cxs"], writes=[fk])
                    rms_rstd(Fb[s_][:], [fk], D, EPS)
                    T.op("dve", lambda v, s_=s_: v.scalar_tensor_tensor(out=Fb[s_][:], in0=Fb[s_][:], scalar=rstd[:, 0:1], in1=fnb[:],
                                                                       op0=ALU.mult, op1=ALU.mult),
                         reads=[fk, "rstd", "fnb"], writes=[fk])
                    T.dma("pool", "out%d" % s_, [(out_d[t * 128:(t + 1) * 128, :], Fb[s_][:])], reads=[fk])

        for en in ("pool", "sp", "act", "dve", "pe"):
            T.wait_all(en)
    return nc


def make_in_maps(S, x, p, positions, norm_mix, w_in, conv_w, conv_b, lambda_q1, lambda_k1,
                 lambda_q2, lambda_k2, subln_g, w_out, norm_ple, w_ple_gate, w_ple_proj, final_norm):
    B = x.shape[0]
    NT = S // 128
    f = np.float32
    shared = {
        "gmix": np.ascontiguousarray(np.asarray(norm_mix[0], f).reshape(KD, 128).T),
        "gple": np.ascontiguousarray(np.asarray(norm_ple[0], f).reshape(KD, 128).T),
        "w_in": np.ascontiguousarray(np.asarray(w_in[0], f)),
        "convw": np.ascontiguousarray(np.asarray(conv_w[0], f).T.reshape(4, 128, 3).transpose(1, 0, 2)),
        "convb": np.ascontiguousarray(np.asarray(conv_b[0], f).reshape(4, 128).T),
        "lam4": np.ascontiguousarray(np.stack([np.asarray(lambda_q1[0], f), np.asarray(lambda_k1[0], f),
                                               np.asarray(lambda_q2[0], f), np.asarray(lambda_k2[0], f)])),
        "subg": np.ascontiguousarray(np.asarray(subln_g[0], f)),
        "w_out": np.ascontiguousarray(np.asarray(w_out[0], f)),
        "w_gate": np.ascontiguousarray(np.asarray(w_ple_gate[0], f)),
        "w_ple": np.ascontiguousarray(np.asarray(w_ple_proj[0], f)),
        "fnorm": np.ascontiguousarray(np.asarray(final_norm, f)),
        "ident": np.eye(128, dtype=f),
        "invf": (np.float32(500000.0) ** (-np.arange(8, dtype=f) / np.float32(8))).astype(f),
    }
    maps = []
    for b in range(B):
        m = dict(shared)
        m["x"] = np.ascontiguousarray(np.asarray(x[b], f))
        m["p"] = np.ascontiguousarray(np.asarray(p[0, b], f))
        m["pos_t"] = np.ascontiguousarray(np.asarray(positions[b], np.int32).reshape(NT, 128).T)
        maps.append(m)
    return maps


def kernel(**inputs):
    x = np.asarray(inputs["x"])
    B, S, _ = x.shape
    nc = build_nc(S)
    in_maps = make_in_maps(S, **{k: np.asarray(v) for k, v in inputs.items()})
    res = run_bass_kernel_spmd(nc, in_maps, core_ids=list(range(B)))
    out = np.stack([np.asarray(r["out"], dtype=np.float32) for r in res.results], axis=0)
    return out
```
